# Optimizing a Trainium2 kernel written in Bass

```python
import jax
import jax.numpy as jnp
from jax import lax
import numpy as np

D_MODEL = 2048
BATCH = 2
SEQ = 16384
DEPTH = 4

HEAD_DIM = 128
ROPE_THETA = 10000.0
LN_EPS = 1e-5
Q_BLOCK = 128

NSA_HEADS = D_MODEL // (2 * HEAD_DIM)
NSA_KV_HEADS = 2
CMP_BLOCK = 32
CMP_STRIDE = 16
SEL_BLOCK = 64
N_SELECT = 16
NSA_WINDOW = 512
NSA_Q_BLOCK = 64

SWA_HEADS = D_MODEL // (2 * HEAD_DIM)
SWA_KV_HEADS = 2
SWA_WINDOW = 128

DIL_HEADS = D_MODEL // HEAD_DIM
DIL_PATTERNS = ((128, 1), (512, 4), (2048, 16))

D_FF = 4 * D_MODEL
DN_ALPHA = (2 * DEPTH) ** 0.25
DN_BETA = (8 * DEPTH) ** -0.25
N_EVEN = (DEPTH + 1) // 2
N_ODD = DEPTH // 2

NSA_Q = NSA_HEADS * HEAD_DIM
NSA_KV = NSA_KV_HEADS * HEAD_DIM
SWA_Q = SWA_HEADS * HEAD_DIM
SWA_KV = SWA_KV_HEADS * HEAD_DIM
EVEN_WIDTHS = (NSA_Q, 3 * NSA_KV, 3 * NSA_KV, 3 * NSA_HEADS, SWA_Q, SWA_KV, SWA_KV)
EVEN_IN = sum(EVEN_WIDTHS)
EVEN_OUT = NSA_Q + SWA_Q
DIL_QKV = 3 * DIL_HEADS * HEAD_DIM
ODD_IN = len(DIL_PATTERNS) * DIL_QKV
ODD_OUT = DIL_HEADS * HEAD_DIM

kernel_name = 'hybrid_nsa_swa_dilated_deepnorm'


def _layer_norm(x, g, b):
    xf = x.astype(jnp.float32)
    mu = jnp.mean(xf, axis=-1, keepdims=True)
    var = jnp.mean(jnp.square(xf - mu), axis=-1, keepdims=True)
    y = (xf - mu) * lax.rsqrt(var + LN_EPS) * g.astype(jnp.float32) + b.astype(jnp.float32)
    return y.astype(x.dtype)


def _rope_tables(positions):
    inv_freq = ROPE_THETA ** (-jnp.arange(0, HEAD_DIM, 2, dtype=jnp.float32) / HEAD_DIM)
    ang = positions.astype(jnp.float32)[..., None] * inv_freq
    return jnp.cos(ang)[:, :, None, :], jnp.sin(ang)[:, :, None, :]


def _rope(t, cos, sin):
    tf = t.astype(jnp.float32)
    t1, t2 = jnp.split(tf, 2, axis=-1)
    return jnp.concatenate([t1 * cos - t2 * sin, t2 * cos + t1 * sin], axis=-1).astype(t.dtype)


def _banded_attention(q, k, v, window, sinks=None):
    B, Hq, L, dh = q.shape
    Hkv = k.shape[1]
    grp = Hq // Hkv
    blk = Q_BLOCK
    nb = -(-L // blk)
    Lp = nb * blk
    span = window - 1 + blk
    qp = jnp.pad(q, ((0, 0), (0, 0), (0, Lp - L), (0, 0))).reshape(B, Hkv, grp, Lp, dh)
    kp = jnp.pad(k, ((0, 0), (0, 0), (window - 1, Lp - L), (0, 0)))
    vp = jnp.pad(v, ((0, 0), (0, 0), (window - 1, Lp - L), (0, 0)))
    rel = jnp.arange(span)[None, :] - jnp.arange(blk)[:, None] - (window - 1)
    band = (rel <= 0) & (rel > -window)
    sink = None if sinks is None else sinks.astype(jnp.float32).reshape(1, Hkv, grp, 1)
    scale = dh ** -0.5

    def step(b):
        start = b * blk
        qb = lax.dynamic_slice_in_dim(qp, start, blk, axis=3)
        kb = lax.dynamic_slice_in_dim(kp, start, span, axis=2)
        vb = lax.dynamic_slice_in_dim(vp, start, span, axis=2)
        kpos = start - (window - 1) + jnp.arange(span)
        mask = band & (kpos >= 0)[None, :]
        s = jnp.einsum('bkgqd,bksd->bkgqs', qb, kb).astype(jnp.float32) * scale
        s = jnp.where(mask, s, -jnp.inf)
        m = jnp.max(s, axis=-1)
        if sink is not None:
            m = jnp.maximum(m, sink)
        p = jnp.exp(s - m[..., None])
        l = jnp.sum(p, axis=-1)
        if sink is not None:
            l = l + jnp.exp(sink - m)
        o = jnp.einsum('bkgqs,bksd->bkgqd', p, vb.astype(jnp.float32)) / l[..., None]
        return o.astype(q.dtype), m + jnp.log(l)

    o, lse = lax.map(step, jnp.arange(nb))
    o = jnp.moveaxis(o, 0, 3).reshape(B, Hq, Lp, dh)[:, :, :L]
    lse = jnp.moveaxis(lse, 0, 3).reshape(B, Hq, Lp)[:, :, :L]
    return o, lse


def _compress(k, pe, w1, w2):
    n_cmp = (k.shape[2] - CMP_BLOCK) // CMP_STRIDE + 1
    idx = np.arange(n_cmp)[:, None] * CMP_STRIDE + np.arange(CMP_BLOCK)[None, :]
    blocks = k[:, :, idx] + pe
    flat = blocks.reshape(blocks.shape[:3] + (CMP_BLOCK * HEAD_DIM,))
    return jax.nn.gelu(flat @ w1) @ w2


def _cmp_to_sel(n_cmp, n_blk):
    pos = np.arange(n_cmp)[:, None] * CMP_STRIDE + np.arange(CMP_BLOCK)[None, :]
    owner = pos // SEL_BLOCK
    m = (owner[:, :, None] == np.arange(n_blk)[None, None, :]).sum(axis=1) / CMP_BLOCK
    return jnp.asarray(m, dtype=jnp.float32)


def _nsa(q, k_cmp, v_cmp, k_slc, v_slc, k_win, v_win, gates, pe_k, pe_v, ck_w1, ck_w2, cv_w1, cv_w2):
    B, H, T, dh = q.shape
    G = k_cmp.shape[1]
    grp = H // G
    kc = _compress(k_cmp, pe_k, ck_w1, ck_w2)
    vc = _compress(v_cmp, pe_v, cv_w1, cv_w2)
    n_cmp = kc.shape[2]
    n_blk = T // SEL_BLOCK
    n_sel = min(N_SELECT, n_blk)
    cmp_end = jnp.arange(n_cmp) * CMP_STRIDE + (CMP_BLOCK - 1)
    cmp_to_sel = _cmp_to_sel(n_cmp, n_blk)
    ks = k_slc.reshape(B, G, n_blk, SEL_BLOCK, dh)
    vs = v_slc.reshape(B, G, n_blk, SEL_BLOCK, dh)
    q5 = q.reshape(B, G, grp, T, dh)
    bi = jnp.arange(B)[:, None, None, None]
    gi = jnp.arange(G)[None, :, None, None]
    blk_id = jnp.arange(n_blk)
    in_blk = jnp.arange(SEL_BLOCK)
    scale = dh ** -0.5

    def step(b):
        q0 = b * NSA_Q_BLOCK
        qb = lax.dynamic_slice_in_dim(q5, q0, NSA_Q_BLOCK, axis=3)
        qpos = q0 + jnp.arange(NSA_Q_BLOCK)
        s = jnp.einsum('bghqd,bgcd->bghqc', qb, kc).astype(jnp.float32) * scale
        s = jnp.where(cmp_end[None, :] <= qpos[:, None], s, -jnp.inf)
        m = jnp.max(s, axis=-1, keepdims=True)
        m = jnp.where(jnp.isfinite(m), m, 0.0)
        p = jnp.exp(s - m)
        l = jnp.sum(p, axis=-1, keepdims=True)
        p = p / jnp.where(l > 0, l, 1.0)
        o_cmp = jnp.einsum('bghqc,bgcd->bghqd', p, vc.astype(jnp.float32))
        imp = jnp.einsum('bghqc,cn->bgqn', p, cmp_to_sel)
        cur = qpos // SEL_BLOCK
        valid = blk_id[None, :] * SEL_BLOCK <= qpos[:, None]
        forced = (blk_id[None, :] == 0) | (blk_id[None, :] == cur[:, None]) | (blk_id[None, :] == cur[:, None] - 1)
        score = jnp.where(forced, jnp.inf, jnp.where(valid, imp, -jnp.inf))
        _, idx = lax.top_k(score, n_sel)
        sk = ks[bi, gi, idx]
        sv = vs[bi, gi, idx]
        s2 = jnp.einsum('bghqd,bgqnld->bghqnl', qb, sk).astype(jnp.float32) * scale
        kpos = idx[..., None] * SEL_BLOCK + in_blk
        s2 = jnp.where((kpos <= qpos[None, None, :, None, None])[:, :, None], s2, -jnp.inf)
        p2 = jax.nn.softmax(s2.reshape(s2.shape[:4] + (n_sel * SEL_BLOCK,)), axis=-1).reshape(s2.shape)
        o_slc = jnp.einsum('bghqnl,bgqnld->bghqd', p2, sv.astype(jnp.float32))
        return o_cmp.astype(q.dtype), o_slc.astype(q.dtype)

    o_cmp, o_slc = lax.map(step, jnp.arange(T // NSA_Q_BLOCK))
    o_cmp = jnp.moveaxis(o_cmp, 0, 3).reshape(B, H, T, dh)
    o_slc = jnp.moveaxis(o_slc, 0, 3).reshape(B, H, T, dh)
    o_win, _ = _banded_attention(q, k_win, v_win, NSA_WINDOW)
    return gates[..., 0:1] * o_cmp + gates[..., 1:2] * o_slc + gates[..., 2:3] * o_win


def _even_mixer(x, cos, sin, w_in, w_o, pe_k, pe_v, ck_w1, ck_w2, cv_w1, cv_w2, sinks):
    B, T, _ = x.shape
    dh = HEAD_DIM
    G = NSA_KV_HEADS
    aq, ak, av, ag, bq, bk, bv = jnp.split(x @ w_in, np.cumsum(EVEN_WIDTHS)[:-1].tolist(), axis=-1)
    aq = _rope(aq.reshape(B, T, NSA_HEADS, dh), cos, sin).transpose(0, 2, 1, 3)
    ak = _rope(ak.reshape(B, T, 3 * G, dh), cos, sin).reshape(B, T, 3, G, dh).transpose(2, 0, 3, 1, 4)
    av = av.reshape(B, T, 3, G, dh).transpose(2, 0, 3, 1, 4)
    gates = jax.nn.sigmoid(ag.reshape(B, T, NSA_HEADS, 3).astype(jnp.float32)).transpose(0, 2, 1, 3).astype(x.dtype)
    o_a = _nsa(aq, ak[0], av[0], ak[1], av[1], ak[2], av[2], gates, pe_k, pe_v, ck_w1, ck_w2, cv_w1, cv_w2)
    bq = _rope(bq.reshape(B, T, SWA_HEADS, dh), cos, sin).transpose(0, 2, 1, 3)
    bk = _rope(bk.reshape(B, T, SWA_KV_HEADS, dh), cos, sin).transpose(0, 2, 1, 3)
    bv = bv.reshape(B, T, SWA_KV_HEADS, dh).transpose(0, 2, 1, 3)
    o_b, _ = _banded_attention(bq, bk, bv, SWA_WINDOW, sinks)
    o = jnp.concatenate([o_a, o_b], axis=1)
    return o.transpose(0, 2, 1, 3).reshape(B, T, EVEN_OUT) @ w_o


def _odd_mixer(x, cos, sin, w_in, w_o):
    B, T, _ = x.shape
    H, dh = DIL_HEADS, HEAD_DIM
    w_groups = w_in.reshape(w_in.shape[0], len(DIL_PATTERNS), DIL_QKV)
    outs, lses = [], []
    for g, (window, dil) in enumerate(DIL_PATTERNS):
        qkv = (x @ w_groups[:, g]).reshape(B, T, 3, H, dh)
        q = _rope(qkv[:, :, 0], cos, sin)
        k = _rope(qkv[:, :, 1], cos, sin)
        v = qkv[:, :, 2]
        u = T // dil
        dec = lambda t: t.reshape(B, u, dil, H, dh).transpose(0, 3, 2, 1, 4).reshape(B, H * dil, u, dh)
        o, lse = _banded_attention(dec(q), dec(k), dec(v), window // dil + 1)
        outs.append(o.reshape(B, H, dil, u, dh).transpose(0, 1, 3, 2, 4).reshape(B, H, T, dh))
        lses.append(lse.reshape(B, H, dil, u).transpose(0, 1, 3, 2).reshape(B, H, T))
    mix = jax.nn.softmax(jnp.stack(lses), axis=0)
    o = jnp.einsum('gbht,gbhtd->bhtd', mix, jnp.stack(outs).astype(jnp.float32)).astype(x.dtype)
    return o.transpose(0, 2, 1, 3).reshape(B, T, ODD_OUT) @ w_o


def _mlp(x, w1, w2):
    return jnp.square(jax.nn.relu(x @ w1)) @ w2


def setup_inputs(seed: int = 0) -> dict:
    key = jax.random.key(seed)
    ks = jax.random.split(key, 19)

    def nrm(k, shape, scale):
        return jax.random.normal(k, shape, dtype=jnp.float32) * scale

    beta = DN_BETA
    x = nrm(ks[0], (BATCH, SEQ, D_MODEL), 1.0)
    offset = jax.random.randint(ks[1], (BATCH, 1), 0, 4096, dtype=jnp.int32)
    positions = offset + jnp.arange(SEQ, dtype=jnp.int32)[None, :]
    e_col = np.concatenate([np.ones(NSA_Q), np.ones(3 * NSA_KV), np.full(3 * NSA_KV, beta), np.ones(3 * NSA_HEADS), np.ones(SWA_Q), np.ones(SWA_KV), np.full(SWA_KV, beta)]).astype(np.float32)
    o_col = np.tile(np.concatenate([np.ones(2 * DIL_HEADS * HEAD_DIM), np.full(DIL_HEADS * HEAD_DIM, beta)]), len(DIL_PATTERNS)).astype(np.float32)
    e_w_in = nrm(ks[2], (N_EVEN, D_MODEL, EVEN_IN), D_MODEL ** -0.5) * jnp.asarray(e_col)
    e_w_o = nrm(ks[3], (N_EVEN, EVEN_OUT, D_MODEL), EVEN_OUT ** -0.5 * beta)
    nsa_pe_k = nrm(ks[4], (N_EVEN, CMP_BLOCK, HEAD_DIM), 0.5)
    nsa_pe_v = nrm(ks[5], (N_EVEN, CMP_BLOCK, HEAD_DIM), 0.5)
    nsa_ck_w1 = nrm(ks[6], (N_EVEN, CMP_BLOCK * HEAD_DIM, HEAD_DIM), (CMP_BLOCK * HEAD_DIM) ** -0.5)
    nsa_ck_w2 = nrm(ks[7], (N_EVEN, HEAD_DIM, HEAD_DIM), HEAD_DIM ** -0.5)
    nsa_cv_w1 = nrm(ks[8], (N_EVEN, CMP_BLOCK * HEAD_DIM, HEAD_DIM), (CMP_BLOCK * HEAD_DIM) ** -0.5)
    nsa_cv_w2 = nrm(ks[9], (N_EVEN, HEAD_DIM, HEAD_DIM), HEAD_DIM ** -0.5)
    swa_sinks = nrm(ks[10], (N_EVEN, SWA_HEADS), 1.0)
    o_w_in = nrm(ks[11], (N_ODD, D_MODEL, ODD_IN), D_MODEL ** -0.5) * jnp.asarray(o_col)
    o_w_o = nrm(ks[12], (N_ODD, ODD_OUT, D_MODEL), ODD_OUT ** -0.5 * beta)
    ln1_g = 1.0 + nrm(ks[13], (DEPTH, D_MODEL), 0.02)
    ln1_b = nrm(ks[14], (DEPTH, D_MODEL), 0.02)
    mlp_w1 = nrm(ks[15], (DEPTH, D_MODEL, D_FF), D_MODEL ** -0.5 * beta)
    mlp_w2 = nrm(ks[16], (DEPTH, D_FF, D_MODEL), D_FF ** -0.5 * beta)
    ln2_g = 1.0 + nrm(ks[17], (DEPTH, D_MODEL), 0.02)
    ln2_b = nrm(ks[18], (DEPTH, D_MODEL), 0.02)
    return {'x': x, 'positions': positions, 'e_w_in': e_w_in, 'e_w_o': e_w_o,
            'nsa_pe_k': nsa_pe_k, 'nsa_pe_v': nsa_pe_v, 'nsa_ck_w1': nsa_ck_w1, 'nsa_ck_w2': nsa_ck_w2,
            'nsa_cv_w1': nsa_cv_w1, 'nsa_cv_w2': nsa_cv_w2, 'swa_sinks': swa_sinks,
            'o_w_in': o_w_in, 'o_w_o': o_w_o, 'ln1_g': ln1_g, 'ln1_b': ln1_b,
            'mlp_w1': mlp_w1, 'mlp_w2': mlp_w2, 'ln2_g': ln2_g, 'ln2_b': ln2_b}


def reference(x, positions, e_w_in, e_w_o, nsa_pe_k, nsa_pe_v, nsa_ck_w1, nsa_ck_w2, nsa_cv_w1, nsa_cv_w2, swa_sinks, o_w_in, o_w_o, ln1_g, ln1_b, mlp_w1, mlp_w2, ln2_g, ln2_b):
    cos, sin = _rope_tables(positions)
    for layer in range(DEPTH):
        i = layer // 2
        if layer % 2 == 0:
            y = _even_mixer(x, cos, sin, e_w_in[i], e_w_o[i], nsa_pe_k[i], nsa_pe_v[i], nsa_ck_w1[i], nsa_ck_w2[i], nsa_cv_w1[i], nsa_cv_w2[i], swa_sinks[i])
        else:
            y = _odd_mixer(x, cos, sin, o_w_in[i], o_w_o[i])
        x = _layer_norm(DN_ALPHA * x + y, ln1_g[layer], ln1_b[layer])
        x = _layer_norm(DN_ALPHA * x + _mlp(x, mlp_w1[layer], mlp_w2[layer]), ln2_g[layer], ln2_b[layer])
    return x
```

```python
import numpy as np
import ml_dtypes
from contextlib import ExitStack
import concourse.bass as bass
import concourse.mybir as mybir
from concourse.bass_utils import run_bass_kernel_spmd

F32 = mybir.dt.float32
BF16 = mybir.dt.bfloat16
I32 = mybir.dt.int32
AF = mybir.ActivationFunctionType
ALU = mybir.AluOpType
AX = mybir.AxisListType

D = 2048
DFF = 8192
DH = 128
DEPTH = 4
ALPHA = float((2 * DEPTH) ** 0.25)
LN_EPS = 1e-5
NEG = -30000.0
SCALE = float(DH ** -0.5)
NCORES = 8
SEQ = 16384
CH = 4096


class Res:
    __slots__ = ("w", "r", "name")

    def __init__(self, name=""):
        self.w = None
        self.r = []
        self.name = name


class Tile:
    def __init__(self, t, name):
        self.t = t
        self.res = Res(name)

    def __getitem__(self, k):
        return self.t[k]


class Op:
    __slots__ = ("eng", "fn", "reads", "writes", "deps", "signal", "done", "pos", "is_dma")


def _res(x):
    return x.res if isinstance(x, Tile) else x


class Prog:
    LIMIT = 16000
    NDMA = 12

    def __init__(self, nc):
        self.nc = nc
        self.ops = []
        self.stack = ExitStack()
        self.h = {"pe": nc.tensor, "act": nc.scalar, "dve": nc.vector, "pool": nc.gpsimd, "sp": nc.sync}
        self.nsem = 0

    def sem(self):
        self.nsem += 1
        return self.stack.enter_context(self.nc.semaphore(f"s{self.nsem}"))

    def sbuf(self, name, shape, dt):
        return Tile(self.stack.enter_context(self.nc.sbuf_tensor("sb_" + name, list(shape), dt)), name)

    def psum(self, name, shape, dt):
        return Tile(self.stack.enter_context(self.nc.psum_tensor("ps_" + name, list(shape), dt)), name)

    def op(self, eng, fn, reads=(), writes=(), dma=False):
        o = Op()
        o.eng = eng
        o.fn = fn
        o.reads = [_res(x) for x in reads]
        o.writes = [_res(x) for x in writes]
        o.signal = False
        o.done = None
        o.is_dma = dma
        self.ops.append(o)
        return o

    def dma(self, q, out, in_, reads=(), writes=()):
        return self.op(q, lambda h: h.dma_start(out=out, in_=in_), reads, writes, dma=True)

    def finish(self):
        ops = self.ops
        cnt = {e: 0 for e in self.h}
        dcnt = {e: 0 for e in self.h}
        known = {e: {} for e in self.h}
        for o in ops:
            if o.is_dma:
                o.pos = dcnt[o.eng]
                dcnt[o.eng] += 1
            else:
                o.pos = cnt[o.eng]
                cnt[o.eng] += 1
            cand = []
            for r in o.reads:
                if r.w is not None:
                    cand.append(r.w)
            for r in o.writes:
                if r.w is not None:
                    cand.append(r.w)
                cand.extend(r.r)
            deps = []
            kn = known[o.eng]
            for d in cand:
                if d is o:
                    continue
                if d.is_dma:
                    key = ("d", d.eng, d.pos % self.NDMA)
                    lvl = d.pos // self.NDMA
                else:
                    if d.eng == "pe" and o.eng == "pe" and not o.is_dma:
                        continue
                    key = ("c", d.eng)
                    lvl = d.pos
                if kn.get(key, -1) >= lvl:
                    continue
                kn[key] = lvl
                d.signal = True
                deps.append(d)
            o.deps = deps
            for r in o.reads:
                r.r.append(o)
            for r in o.writes:
                r.w = o
                r.r = []
        csem = {e: [] for e in self.h}
        csig = {e: 0 for e in self.h}
        dsem = {e: [] for e in self.h}
        dk = {e: 0 for e in self.h}
        last_dma = {}
        for o in ops:
            h = self.h[o.eng]
            for d in o.deps:
                s, v = d.done
                h.wait_ge(s, v)
            if o.is_dma:
                k = dk[o.eng]
                dk[o.eng] += 1
                slot = k % self.NDMA
                lvl = k // self.NDMA
                if len(dsem[o.eng]) <= slot:
                    dsem[o.eng].append(self.sem())
                s = dsem[o.eng][slot]
                if lvl > 0:
                    h.wait_ge(s, 16 * lvl)
                o.fn(h).then_inc(s, 16)
                o.done = (s, 16 * (lvl + 1))
                last_dma[(o.eng, slot)] = o.done
            else:
                ins = o.fn(h)
                if o.signal:
                    n = csig[o.eng]
                    csig[o.eng] += 1
                    ep = n // self.LIMIT
                    if len(csem[o.eng]) <= ep:
                        csem[o.eng].append(self.sem())
                    s = csem[o.eng][ep]
                    ins.then_inc(s, 1)
                    o.done = (s, n % self.LIMIT + 1)
            o.fn = None
        for (e, slot), (s, v) in last_dma.items():
            self.nc.sync.wait_ge(s, v)
        self.stack.close()
        self.ops = []


class Ctx:
    def __init__(self, P, ident_dram):
        self.P = P
        nc = P.nc
        self.acc = [P.psum(f"acc{i}", [128, 512], F32) for i in range(6)]
        self.tp = [P.psum(f"tp{i}", [128, 1024], BF16) for i in range(2)]
        self.ident = P.sbuf("ident", [128, 128], BF16)
        P.dma("pool", self.ident[:, :], ident_dram, writes=[self.ident])
        self.wb = [P.sbuf(f"wb{i}", [128, 16, 512], BF16) for i in range(2)]
        self.wi = 0
        self.ai = 0
        self.ti = 0

    def next_acc(self):
        a = self.acc[self.ai % len(self.acc)]
        self.ai += 1
        return a

    def next_tp(self):
        a = self.tp[self.ti % 2]
        self.ti += 1
        return a

    def load_w(self, w_dram, k0, c0, ncols=512, nk=16):
        wb = self.wb[self.wi % 2]
        self.wi += 1
        src = w_dram[k0 * 128:(k0 + nk) * 128, c0:c0 + ncols].rearrange("(k p) c -> p k c", p=128)
        self.P.dma("pool", wb[:, 0:nk, 0:ncols], src, writes=[wb])
        return wb

    def transpose_to(self, dstT, dst_res, src, src_res, ncol_tiles, tok0, dst_reads=()):
        P = self.P
        for j0 in range(0, ncol_tiles, 8):
            n = min(8, ncol_tiles - j0)
            tp = self.next_tp()
            for j in range(n):
                P.op("pe", lambda h, tp=tp, j=j, jj=j0 + j: h.transpose(tp[:, j * 128:(j + 1) * 128],
                                                                        src[:, jj * 128:(jj + 1) * 128],
                                                                        self.ident[:, :]),
                     reads=[src_res, self.ident], writes=[tp])
            P.op("act", lambda h, tp=tp, n=n, j0=j0: h.activation(
                out=dstT[:, j0:j0 + n, tok0:tok0 + 128],
                in_=tp[:, 0:n * 128].rearrange("p (j t) -> p j t", t=128), func=AF.Copy),
                 reads=[tp], writes=[dst_res])


def layer_norm(P, buf, res, g_t, b_t, stat, eng_aff="pool"):
    st, mv, rs = stat
    for c in range(4):
        P.op("dve", lambda h, c=c: h.bn_stats(st[:, c, :], buf[:, c * 512:(c + 1) * 512]), reads=[res], writes=[st])
    P.op("dve", lambda h: h.bn_aggr(mv[:, :], st[:, :, :].rearrange("p a b -> p (a b)")), reads=[st], writes=[mv])
    P.op("dve", lambda h: h.tensor_scalar(rs[:, :], mv[:, 1:2], LN_EPS, None, ALU.add), reads=[mv], writes=[rs])
    P.op("act", lambda h: h.activation(out=rs[:, :], in_=rs[:, :], func=AF.Sqrt), reads=[rs], writes=[rs])
    P.op("dve", lambda h: h.reciprocal(rs[:, :], rs[:, :]), reads=[rs], writes=[rs])
    P.op("dve", lambda h: h.tensor_scalar(buf[:, :], buf[:, :], mv[:, 0:1], rs[:, 0:1], ALU.subtract, ALU.mult),
         reads=[res, mv, rs], writes=[res])
    P.op(eng_aff, lambda h: h.tensor_tensor(buf[:, :], buf[:, :], g_t[:, :], ALU.mult), reads=[res, g_t], writes=[res])
    P.op("dve", lambda h: h.tensor_tensor(buf[:, :], buf[:, :], b_t[:, :], ALU.add), reads=[res, b_t], writes=[res])


def build_M(NT, o_fm=False):
    nc = bass.Bass("TRN2", target_bir_lowering=False)
    dt = nc.dram_tensor
    if o_fm:
        o_d = dt("o", [16, 128, NT], BF16, kind="ExternalInput").ap()
    else:
        o_d = dt("o", [NT, D], BF16, kind="ExternalInput").ap()
    x_d = dt("x", [NT, D], F32, kind="ExternalInput").ap()
    wo_d = dt("w_o", [D, D], F32, kind="ExternalInput").ap()
    w1_d = dt("w1", [D, DFF], F32, kind="ExternalInput").ap()
    w2_d = dt("w2", [DFF, D], F32, kind="ExternalInput").ap()
    lnp_d = dt("lnp", [4, D], F32, kind="ExternalInput").ap()
    id_d = dt("ident", [128, 128], F32, kind="ExternalInput").ap()
    y_d = dt("y", [NT, D], F32, kind="ExternalOutput").ap()
    P = Prog(nc)
    C = Ctx(P, id_d)
    TB = 512 if NT % 512 == 0 else NT
    nt = TB // 128
    lnp = [P.sbuf(f"lnp{i}", [128, D], F32) for i in range(4)]
    for i in range(4):
        P.dma("sp", lnp[i][:, :], lnp_d[i, :].partition_broadcast(128), writes=[lnp[i]])
    ob = [P.sbuf(f"ob{i}", [128, D], BF16) for i in range(2)]
    xb = [P.sbuf(f"xb{i}", [128, D], F32) for i in range(nt)]
    xs = P.sbuf("xs", [128, D], BF16)
    xT = P.sbuf("xT", [128, 16, TB], BF16)
    hT = P.sbuf("hT", [128, 64, TB], BF16)
    ht = [P.sbuf(f"ht{i}", [128, TB], F32) for i in range(2)]
    stat = (P.sbuf("st", [128, 4, 6], F32), P.sbuf("mv", [128, 2], F32), P.sbuf("rs", [128, 1], F32))
    oi = 0
    for blk in range(NT // TB):
        t0 = blk * TB
        if o_fm:
            P.dma("sp", xT[:, :, :], o_d[:, :, t0:t0 + TB].rearrange("k p t -> p k t"), writes=[xT])
        for ti in range(nt):
            P.dma("sp", xb[ti][:, :], x_d[t0 + ti * 128:t0 + (ti + 1) * 128, :], writes=[xb[ti]])
            if not o_fm:
                o_t = ob[oi % 2]
                oi += 1
                P.dma("sp", o_t[:, :], o_d[t0 + ti * 128:t0 + (ti + 1) * 128, :], writes=[o_t])
                C.transpose_to(xT, xT.res, o_t, o_t.res, 16, ti * 128)
        for n in range(4):
            wb = C.load_w(wo_d, 0, n * 512)
            for ti in range(nt):
                acc = C.next_acc()
                for k in range(16):
                    P.op("pe", lambda h, acc=acc, k=k, ti=ti, wb=wb: h.matmul(
                        acc[:, :], xT[:, k, ti * 128:(ti + 1) * 128], wb[:, k, :], start=(k == 0), stop=(k == 15)),
                         reads=[xT, wb], writes=[acc])
                P.op("dve", lambda h, acc=acc, ti=ti, n=n: h.scalar_tensor_tensor(
                    xb[ti][:, n * 512:(n + 1) * 512], xb[ti][:, n * 512:(n + 1) * 512], ALPHA, acc[:, :],
                    ALU.mult, ALU.add), reads=[xb[ti], acc], writes=[xb[ti]])
        for ti in range(nt):
            layer_norm(P, xb[ti], xb[ti].res, lnp[0], lnp[1], stat)
            P.op("act", lambda h, ti=ti: h.activation(out=xs[:, :], in_=xb[ti][:, :], func=AF.Copy),
                 reads=[xb[ti]], writes=[xs])
            C.transpose_to(xT, xT.res, xs, xs.res, 16, ti * 128)
        hi = 0
        for piece in range(16):
            wb = C.load_w(w1_d, 0, piece * 512)
            for fc in range(4):
                acc = C.next_acc()
                for k in range(16):
                    P.op("pe", lambda h, acc=acc, k=k, fc=fc, wb=wb: h.matmul(
                        acc[:, 0:TB], wb[:, k, fc * 128:(fc + 1) * 128], xT[:, k, :], start=(k == 0), stop=(k == 15)),
                         reads=[xT, wb], writes=[acc])
                tmp = ht[hi % 2]
                hi += 1
                P.op("act", lambda h, acc=acc, tmp=tmp: h.activation(out=tmp[:, :], in_=acc[:, 0:TB], func=AF.Relu),
                     reads=[acc], writes=[tmp])
                P.op("pool", lambda h, tmp=tmp, j=piece * 4 + fc: h.tensor_tensor(hT[:, j, :], tmp[:, :], tmp[:, :], ALU.mult),
                     reads=[tmp], writes=[hT])
        for n in range(4):
            accs = [C.next_acc() for _ in range(nt)]
            for kp in range(4):
                wb = C.load_w(w2_d, kp * 16, n * 512)
                for ti in range(nt):
                    for k in range(16):
                        kk = kp * 16 + k
                        P.op("pe", lambda h, acc=accs[ti], kk=kk, k=k, ti=ti, wb=wb: h.matmul(
                            acc[:, :], hT[:, kk, ti * 128:(ti + 1) * 128], wb[:, k, :], start=(kk == 0), stop=(kk == 63)),
                             reads=[hT, wb], writes=[accs[ti]])
            for ti in range(nt):
                P.op("dve", lambda h, acc=accs[ti], ti=ti, n=n: h.scalar_tensor_tensor(
                    xb[ti][:, n * 512:(n + 1) * 512], xb[ti][:, n * 512:(n + 1) * 512], ALPHA, acc[:, :],
                    ALU.mult, ALU.add), reads=[xb[ti], accs[ti]], writes=[xb[ti]])
        for ti in range(nt):
            layer_norm(P, xb[ti], xb[ti].res, lnp[2], lnp[3], stat)
            P.dma("sp", y_d[t0 + ti * 128:t0 + (ti + 1) * 128, :], xb[ti][:, :], reads=[xb[ti]])
    P.finish()
    return nc


_IDENT = np.eye(128, dtype=np.float32)


def run_M(o_sh, x_sh, w_o, w1, w2, lnp, o_fm=False):
    NT = x_sh[0].shape[0]
    nc = build_M(NT, o_fm)
    in_maps = [{"o": o_sh[c], "x": x_sh[c], "w_o": w_o, "w1": w1, "w2": w2, "lnp": lnp, "ident": _IDENT}
               for c in range(len(x_sh))]
    res = run_bass_kernel_spmd(nc, in_maps, core_ids=list(range(len(x_sh))))
    return [r["y"] for r in res.results]


TWO_PI = 6.283185307179586
C1 = 6.28125
C2 = TWO_PI - C1
PI_SAFE = 3.1415925


def rope_tables(P, pos_d, invf_d, NT):
    posi = P.sbuf("posi", [128, NT], I32)
    ang = P.sbuf("ang", [128, NT], F32)
    kf = P.sbuf("kf", [128, NT], F32)
    ki = P.sbuf("ki", [128, NT], I32)
    tmp = P.sbuf("rtmp", [128, NT], F32)
    cosT = P.sbuf("cosT", [128, NT], F32)
    sinT = P.sbuf("sinT", [128, NT], F32)
    invf = P.sbuf("invf", [128, 1], F32)
    P.dma("sp", invf[:, :], invf_d, writes=[invf])
    P.dma("sp", posi[:, :], pos_d.partition_broadcast(128), writes=[posi])
    P.op("dve", lambda h: h.tensor_copy(ang[:, :], posi[:, :]), reads=[posi], writes=[ang])
    P.op("dve", lambda h: h.tensor_scalar(ang[:, :], ang[:, :], invf[:, 0:1], None, ALU.mult), reads=[ang, invf], writes=[ang])
    P.op("dve", lambda h: h.tensor_scalar(kf[:, :], ang[:, :], 1.0 / TWO_PI, None, ALU.mult), reads=[ang], writes=[kf])
    P.op("dve", lambda h: h.tensor_copy(ki[:, :], kf[:, :]), reads=[kf], writes=[ki])
    P.op("dve", lambda h: h.tensor_copy(kf[:, :], ki[:, :]), reads=[ki], writes=[kf])
    P.op("dve", lambda h: h.scalar_tensor_tensor(ang[:, :], kf[:, :], -C1, ang[:, :], ALU.mult, ALU.add), reads=[kf, ang], writes=[ang])
    P.op("dve", lambda h: h.scalar_tensor_tensor(ang[:, :], kf[:, :], -C2, ang[:, :], ALU.mult, ALU.add), reads=[kf, ang], writes=[ang])

    def wrap(buf):
        P.op("dve", lambda h: h.tensor_scalar(tmp[:, :], buf[:, :], PI_SAFE, -TWO_PI, ALU.is_gt, ALU.mult), reads=[buf], writes=[tmp])
        P.op("dve", lambda h: h.tensor_tensor(buf[:, :], buf[:, :], tmp[:, :], ALU.add), reads=[buf, tmp], writes=[buf])
        P.op("dve", lambda h: h.tensor_scalar(tmp[:, :], buf[:, :], -PI_SAFE, TWO_PI, ALU.is_lt, ALU.mult), reads=[buf], writes=[tmp])
        P.op("dve", lambda h: h.tensor_tensor(buf[:, :], buf[:, :], tmp[:, :], ALU.add), reads=[buf, tmp], writes=[buf])
        P.op("dve", lambda h: h.tensor_scalar(buf[:, :], buf[:, :], PI_SAFE, -PI_SAFE, ALU.min, ALU.max), reads=[buf], writes=[buf])

    wrap(ang)
    P.op("act", lambda h: h.activation(out=sinT[:, :], in_=ang[:, :], func=AF.Sin), reads=[ang], writes=[sinT])
    P.op("dve", lambda h: h.tensor_scalar(ang[:, :], ang[:, :], TWO_PI / 4, None, ALU.add), reads=[ang], writes=[ang])
    wrap(ang)
    P.op("act", lambda h: h.activation(out=cosT[:, :], in_=ang[:, :], func=AF.Sin), reads=[ang], writes=[cosT])
    return cosT, sinT


import os
ADD_ENG = os.environ.get("ADD_ENG", "pool")


def build_P(NT, ncols, rope_chunks, v_pieces, gate_cols):
    nc = bass.Bass("TRN2", target_bir_lowering=False)
    dt = nc.dram_tensor
    x_d = dt("x", [NT, D], F32, kind="ExternalInput").ap()
    w_d = dt("w_in", [D, ncols], F32, kind="ExternalInput").ap()
    pos_d = dt("pos", [NT], I32, kind="ExternalInput").ap()
    invf_d = dt("invf", [128, 1], F32, kind="ExternalInput").ap()
    rt_d = dt("rotT", [128, 128], F32, kind="ExternalInput").ap()
    id_d = dt("ident", [128, 128], F32, kind="ExternalInput").ap()
    nrope = len(rope_chunks)
    nv = sum(n for _, n, _ in v_pieces)
    if nrope:
        qk_d = dt("qkT", [nrope, 128, NT], BF16, kind="ExternalOutput").ap()
    if nv:
        v_d = dt("v", [NT, nv], BF16, kind="ExternalOutput").ap()
    if gate_cols:
        g_d = dt("gates", [NT, gate_cols[1]], F32, kind="ExternalOutput").ap()
    P = Prog(nc)
    C = Ctx(P, id_d)
    cosT, sinT = rope_tables(P, pos_d, invf_d, NT)
    rotT = P.sbuf("rotT", [128, 128], BF16)
    P.dma("pool", rotT[:, :], rt_d, writes=[rotT])
    TB = 512 if NT % 512 == 0 else NT
    nt = TB // 128
    xf = [P.sbuf(f"xf{i}", [128, D], F32) for i in range(2)]
    xs = P.sbuf("xs", [128, D], BF16)
    xT = P.sbuf("xT", [128, 16, TB], BF16)
    qraw = [P.sbuf(f"qraw{i}", [128, TB], BF16) for i in range(2)]
    t1 = [P.sbuf(f"t1{i}", [128, TB], F32) for i in range(2)]
    t2 = [P.sbuf(f"t2{i}", [128, TB], F32) for i in range(2)]
    qo = [P.sbuf(f"qo{i}", [128, TB], BF16) for i in range(2)]
    vo = [P.sbuf(f"vo{i}", [128, 512], BF16) for i in range(2)]
    go = [P.sbuf(f"go{i}", [128, 32], F32) for i in range(2)]
    it = 0
    rpieces = []
    i = 0
    while i < nrope:
        j = i
        while j + 1 < nrope and j + 1 - i < 4 and rope_chunks[j + 1] == rope_chunks[j] + 128:
            j += 1
        rpieces.append((i, j - i + 1))
        i = j + 1
    for blk in range(NT // TB):
        t0 = blk * TB
        for ti in range(nt):
            xt = xf[ti % 2]
            P.dma("sp", xt[:, :], x_d[t0 + ti * 128:t0 + (ti + 1) * 128, :], writes=[xt])
            P.op("act", lambda h, xt=xt: h.activation(out=xs[:, :], in_=xt[:, :], func=AF.Copy), reads=[xt], writes=[xs])
            C.transpose_to(xT, xT.res, xs, xs.res, 16, ti * 128)
        for (ci0, nch) in rpieces:
            wb = C.load_w(w_d, 0, rope_chunks[ci0], ncols=nch * 128)
            for c in range(nch):
                acc = C.next_acc()
                for k in range(16):
                    P.op("pe", lambda h, acc=acc, k=k, c=c, wb=wb: h.matmul(
                        acc[:, 0:TB], wb[:, k, c * 128:(c + 1) * 128], xT[:, k, :], start=(k == 0), stop=(k == 15)),
                         reads=[xT, wb], writes=[acc])
                b = it % 2
                it += 1
                P.op("act", lambda h, acc=acc, b=b: h.activation(out=qraw[b][:, :], in_=acc[:, 0:TB], func=AF.Copy),
                     reads=[acc], writes=[qraw[b]])
                acc2 = C.next_acc()
                P.op("pe", lambda h, acc2=acc2, b=b: h.matmul(acc2[:, 0:TB], rotT[:, :], qraw[b][:, :], start=True, stop=True),
                     reads=[rotT, qraw[b]], writes=[acc2])
                P.op("dve", lambda h, acc=acc, b=b, t0=t0: h.tensor_tensor(t1[b][:, :], acc[:, 0:TB], cosT[:, t0:t0 + TB], ALU.mult),
                     reads=[acc, cosT, qraw[b]], writes=[t1[b]])
                P.op("dve", lambda h, acc2=acc2, b=b, t0=t0: h.tensor_tensor(t2[b][:, :], acc2[:, 0:TB], sinT[:, t0:t0 + TB], ALU.mult),
                     reads=[acc2, sinT], writes=[t2[b]])
                P.op(ADD_ENG, lambda h, b=b: h.tensor_tensor(qo[b][:, :], t1[b][:, :], t2[b][:, :], ALU.add),
                     reads=[t1[b], t2[b]], writes=[qo[b]])
                P.dma("sp", qk_d[ci0 + c, :, t0:t0 + TB], qo[b][:, :], reads=[qo[b]])
        for (c0, n, vc0) in v_pieces:
            wb = C.load_w(w_d, 0, c0, ncols=n)
            for ti in range(nt):
                acc = C.next_acc()
                for k in range(16):
                    P.op("pe", lambda h, acc=acc, k=k, ti=ti, wb=wb, n=n: h.matmul(
                        acc[:, 0:n], xT[:, k, ti * 128:(ti + 1) * 128], wb[:, k, 0:n], start=(k == 0), stop=(k == 15)),
                         reads=[xT, wb], writes=[acc])
                b = it % 2
                it += 1
                P.op("act", lambda h, acc=acc, b=b, n=n: h.activation(out=vo[b][:, 0:n], in_=acc[:, 0:n], func=AF.Copy),
                     reads=[acc], writes=[vo[b]])
                P.dma("sp", v_d[t0 + ti * 128:t0 + (ti + 1) * 128, vc0:vc0 + n], vo[b][:, 0:n], reads=[vo[b]])
        if gate_cols:
            c0, n = gate_cols
            wb = C.load_w(w_d, 0, c0, ncols=n)
            for ti in range(nt):
                acc = C.next_acc()
                for k in range(16):
                    P.op("pe", lambda h, acc=acc, k=k, ti=ti, wb=wb, n=n: h.matmul(
                        acc[:, 0:n], xT[:, k, ti * 128:(ti + 1) * 128], wb[:, k, 0:n], start=(k == 0), stop=(k == 15)),
                         reads=[xT, wb], writes=[acc])
                b = it % 2
                it += 1
                P.op("act", lambda h, acc=acc, b=b, n=n: h.activation(out=go[b][:, 0:n], in_=acc[:, 0:n], func=AF.Sigmoid),
                     reads=[acc], writes=[go[b]])
                P.dma("sp", g_d[t0 + ti * 128:t0 + (ti + 1) * 128, :], go[b][:, 0:n], reads=[go[b]])
    P.finish()
    return nc


def _rot_T():
    r = np.zeros((128, 128), np.float32)
    for m in range(64):
        r[m + 64, m] = -1.0
    for m in range(64, 128):
        r[m - 64, m] = 1.0
    return r


_ROTT = _rot_T()
_INVF = (10000.0 ** (-np.arange(0, 128, 2, dtype=np.float32) / np.float32(128))).astype(np.float32)
_INVF128 = np.concatenate([_INVF, _INVF]).reshape(128, 1).astype(np.float32)

ODD_ROPE = [g * 6144 + qk * 2048 + h * 128 for g in range(3) for qk in range(2) for h in range(16)]
ODD_V = [(g * 6144 + 4096 + j * 512, 512, g * 2048 + j * 512) for g in range(3) for j in range(4)]
EVEN_ROPE = [c * 128 for c in range(14)] + [2584 + c * 128 for c in range(10)]
EVEN_V = [(1792, 512, 0), (2304, 256, 512), (3864, 256, 768)]
EVEN_GATE = (2560, 24)


def run_P(x_sh, pos_sh, w_in, rope_chunks, v_pieces, gate_cols):
    NT = x_sh[0].shape[0]
    nc = build_P(NT, w_in.shape[1], rope_chunks, v_pieces, gate_cols)
    in_maps = [{"x": x_sh[c], "pos": pos_sh[c], "w_in": w_in, "invf": _INVF128, "rotT": _ROTT, "ident": _IDENT}
               for c in range(len(x_sh))]
    res = run_bass_kernel_spmd(nc, in_maps, core_ids=list(range(len(x_sh))))
    return res.results


def build_A_odd(NT, HMAX, dils, NH):
    nc = bass.Bass("TRN2", target_bir_lowering=False)
    dt = nc.dram_tensor
    NG = len(dils)
    q_d = dt("qT", [NG, NH, 128, NT], BF16, kind="ExternalInput").ap()
    k_d = dt("kT", [NG, NH, 128, HMAX + NT], BF16, kind="ExternalInput").ap()
    v_d = dt("v", [NG, HMAX + NT, NH * 128], BF16, kind="ExternalInput").ap()
    mk_d = dt("masks", [3, 128, 128], F32, kind="ExternalInput").ap()
    id_d = dt("ident", [128, 128], F32, kind="ExternalInput").ap()
    o_d = dt("oT", [NH, 128, NT], BF16, kind="ExternalOutput").ap()
    P = Prog(nc)
    C = Ctx(P, id_d)
    mk = P.sbuf("mk", [128, 3, 128], BF16)
    P.dma("pool", mk[:, :, :], mk_d.rearrange("m p q -> p m q"), writes=[mk])
    ones = P.sbuf("ones", [128, 128], BF16)
    P.op("dve", lambda h: h.memset(ones[:, :], 1.0), writes=[ones])
    KMAX = HMAX + NT
    Qs = [P.sbuf(f"Qs{i}", [128, NT], BF16) for i in range(2)]
    Ks = [P.sbuf(f"Ks{i}", [128, KMAX], BF16) for i in range(2)]
    Vd = [P.sbuf(f"Vd{i}", [128, KMAX], BF16) for i in range(2)]
    ACC = P.sbuf("ACC", [128, 2, NT], F32)
    oTs = [P.sbuf(f"oTs{i}", [128, NT], BF16) for i in range(2)]
    Pt = [P.sbuf(f"Pt{i}", [128, 2, 128], BF16) for i in range(3)]
    ident = C.ident
    it = 0
    pi = 0
    for hd in range(NH):
        for g, dil in enumerate(dils):
            Hg = 128 * dil
            KL = Hg + NT
            J = KL // (128 * dil)
            b = it % 2
            it += 1
            Q, Kb, V = Qs[b], Ks[b], Vd[b]
            P.dma("sp", Q[:, :], q_d[g, hd, :, :], writes=[Q])
            P.dma("sp", Kb[:, 0:KL], k_d[g, hd, :, HMAX - Hg:HMAX + NT], writes=[Kb])
            Vv = V[:, 0:KL].rearrange("p (c j d) -> p c j d", c=dil, j=J)
            for c in range(dil):
                src = v_d[g, HMAX - Hg + c:HMAX + NT:dil, hd * 128:(hd + 1) * 128].rearrange("(j p) d -> p j d", p=128)
                P.dma("sp", Vv[:, c, :, :], src, writes=[V])
            for c in range(dil):
                for k in range(NT // (128 * dil)):
                    qsl = slice(128 * k * dil + c, 128 * (k + 1) * dil + c - dil + 1, dil)
                    ksl0 = slice(128 * k * dil + c, 128 * (k + 1) * dil + c - dil + 1, dil)
                    ksl1 = slice(128 * (k + 1) * dil + c, 128 * (k + 2) * dil + c - dil + 1, dil)
                    S = C.next_acc()
                    Sv = S[:, 0:256].rearrange("p (a q) -> p a q", a=2)
                    P.op("pe", lambda h, Sv=Sv, Kb=Kb, Q=Q, ksl0=ksl0, qsl=qsl: h.matmul(
                        Sv[:, 0, :], Kb[:, ksl0], Q[:, qsl], start=True, stop=False), reads=[Kb, Q], writes=[S])
                    P.op("pe", lambda h, Sv=Sv, k=k: h.matmul(Sv[:, 0, :], ident[:, :], mk[:, 0, :], start=False, stop=(k != 0)),
                         reads=[ident, mk], writes=[S])
                    if k == 0:
                        P.op("pe", lambda h, Sv=Sv: h.matmul(Sv[:, 0, :], ident[:, :], mk[:, 2, :], start=False, stop=True),
                             reads=[ident, mk], writes=[S])
                    P.op("pe", lambda h, Sv=Sv, Kb=Kb, Q=Q, ksl1=ksl1, qsl=qsl: h.matmul(
                        Sv[:, 1, :], Kb[:, ksl1], Q[:, qsl], start=True, stop=False), reads=[Kb, Q], writes=[S])
                    P.op("pe", lambda h, Sv=Sv: h.matmul(Sv[:, 1, :], ident[:, :], mk[:, 1, :], start=False, stop=True),
                         reads=[ident, mk], writes=[S])
                    pt = Pt[pi % 3]
                    pi += 1
                    P.op("act", lambda h, pt=pt, Sv=Sv: h.activation(out=pt[:, :, :], in_=Sv, func=AF.Exp, scale=SCALE),
                         reads=[S], writes=[pt])
                    OL = C.next_acc()
                    OLv = OL[:, 0:256].rearrange("p (a q) -> p a q", a=2)
                    P.op("pe", lambda h, OLv=OLv, Vv=Vv, c=c, k=k, pt=pt: h.matmul(
                        OLv[:, 0, :], Vv[:, c, k, :], pt[:, 0, :], start=True, stop=False), reads=[V, pt], writes=[OL])
                    P.op("pe", lambda h, OLv=OLv, Vv=Vv, c=c, k=k, pt=pt: h.matmul(
                        OLv[:, 0, :], Vv[:, c, k + 1, :], pt[:, 1, :], start=False, stop=True), reads=[V, pt], writes=[OL])
                    P.op("pe", lambda h, OLv=OLv, pt=pt: h.matmul(OLv[:, 1, :], ones[:, :], pt[:, 0, :], start=True, stop=False),
                         reads=[ones, pt], writes=[OL])
                    P.op("pe", lambda h, OLv=OLv, pt=pt: h.matmul(OLv[:, 1, :], ones[:, :], pt[:, 1, :], start=False, stop=True),
                         reads=[ones, pt], writes=[OL])
                    if g == 0:
                        P.op("dve", lambda h, OLv=OLv, qsl=qsl: h.tensor_copy(ACC[:, :, qsl], OLv), reads=[OL], writes=[ACC])
                    else:
                        P.op("dve", lambda h, OLv=OLv, qsl=qsl: h.tensor_tensor(ACC[:, :, qsl], ACC[:, :, qsl], OLv, ALU.add),
                             reads=[OL, ACC], writes=[ACC])
        ob = oTs[hd % 2]
        P.op("dve", lambda h: h.reciprocal(ACC[:, 1, :], ACC[:, 1, :]), reads=[ACC], writes=[ACC])
        P.op("dve", lambda h, ob=ob: h.tensor_tensor(ob[:, :], ACC[:, 0, :], ACC[:, 1, :], ALU.mult), reads=[ACC], writes=[ob])
        P.dma("sp", o_d[hd, :, :], ob[:, :], reads=[ob])
    P.finish()
    return nc


_UT = np.where(np.arange(128)[:, None] >= np.arange(128)[None, :], 0.0, NEG).astype(np.float32)
_LT = np.where(np.arange(128)[:, None] <= np.arange(128)[None, :], 0.0, NEG).astype(np.float32)


def odd_masks(first_chunk):
    hb = np.full((128, 128), NEG if first_chunk else 0.0, np.float32)
    return np.stack([_UT, _LT, hb])


def build_A_even(NT, SEQ_, WH=4):
    nc = bass.Bass("TRN2", target_bir_lowering=False)
    dt = nc.dram_tensor
    NQ = NT // 128
    NKT = SEQ_ // 128
    NOWN = NT // 128
    NCMP = (SEQ_ - 32) // 16 + 1
    NCT = (NCMP + 127) // 128
    NCP = NCT * 128
    NBLK = SEQ_ // 64
    PB = min(128, NBLK)
    NBH = NBLK // PB
    KPH = PB // 2
    qa_d = dt("qTa", [8, 128, NT], BF16, kind="ExternalInput").ap()
    qb_d = dt("qTb", [8, 128, NT], BF16, kind="ExternalInput").ap()
    kc_d = dt("kcT", [2, 128, SEQ_], BF16, kind="ExternalInput").ap()
    vc_d = dt("vcT", [2, 128, SEQ_], BF16, kind="ExternalInput").ap()
    ks_d = dt("ksT", [2, 128, SEQ_], BF16, kind="ExternalInput").ap()
    vs_d = dt("vs", [2, SEQ_, 128], BF16, kind="ExternalInput").ap()
    kw_d = dt("kwT", [2, 128, WH * 128 + NT], BF16, kind="ExternalInput").ap()
    vw_d = dt("vw", [2, WH * 128 + NT, 128], BF16, kind="ExternalInput").ap()
    kb_d = dt("kbT", [2, 128, 128 + NT], BF16, kind="ExternalInput").ap()
    vb_d = dt("vb", [2, 128 + NT, 128], BF16, kind="ExternalInput").ap()
    g_d = dt("gates", [NT, 24], F32, kind="ExternalInput").ap()
    sk_d = dt("sinks", [8], F32, kind="ExternalInput").ap()
    aadd_d = dt("aadd", [NT, NBLK], F32, kind="ExternalInput").ap()
    c2s_d = dt("c2s", [NCP, NBLK], F32, kind="ExternalInput").ap()
    cmb_d = dt("cmb", [NCT, 128, NT], BF16, kind="ExternalInput").ap()
    ind_d = dt("ind", [KPH, PB, 128], F32, kind="ExternalInput").ap()
    mk_d = dt("masks4", [3, 128, 512], F32, kind="ExternalInput").ap()
    pe_d = dt("peT", [2, 128, 32], F32, kind="ExternalInput").ap()
    w1_d = dt("cw1", [2, 4096, 128], F32, kind="ExternalInput").ap()
    w2_d = dt("cw2", [2, 128, 128], F32, kind="ExternalInput").ap()
    id_d = dt("ident", [128, 128], F32, kind="ExternalInput").ap()
    o_d = dt("o", [NT, D], BF16, kind="ExternalOutput").ap()
    P = Prog(nc)
    bank = [P.psum(f"bk{i}", [128, 512], F32) for i in range(8)]
    Sb = bank[0:3]
    OA, OB, IA, IB, MB = bank[3], bank[4], bank[5], bank[6], bank[7]
    identb = P.sbuf("identb", [128, 128], BF16)
    identf = P.sbuf("identf", [128, 128], F32)
    P.dma("pool", identb[:, :], id_d, writes=[identb])
    P.dma("sp", identf[:, :], id_d, writes=[identf])
    mk4 = P.sbuf("mk4", [128, 3, 512], BF16)
    P.dma("pool", mk4[:, :, :], mk_d.rearrange("m p q -> p m q"), writes=[mk4])
    ind = P.sbuf("ind", [PB, KPH, 128], BF16)
    P.dma("pool", ind[:, :, :], ind_d.rearrange("k p q -> p k q"), writes=[ind])
    c2s = P.sbuf("c2s", [128, NCT, NBLK], BF16)
    P.dma("pool", c2s[:, :, :], c2s_d.rearrange("(k p) n -> p k n", p=128), writes=[c2s])
    sk = P.sbuf("sk", [128, 8], F32)
    P.dma("sp", sk[:, :], sk_d.partition_broadcast(128), writes=[sk])
    P.op("act", lambda h: h.activation(out=sk[:, :], in_=sk[:, :], func=AF.Exp), reads=[sk], writes=[sk])

    kcT = P.sbuf("kcT", [128, 2, NCP], BF16)
    vcA = P.sbuf("vcA", [128, 2, NCT, 129], BF16)
    P.op("dve", lambda h: h.memset(kcT[:, :, :], 0.0), writes=[kcT])
    P.op("dve", lambda h: h.memset(vcA[:, :, :, :], 0.0), writes=[vcA])
    P.op("dve", lambda h: h.memset(vcA[:, :, :, 128:129], 1.0), writes=[vcA])
    ksl = P.sbuf("ksl", [128, SEQ_], BF16)
    src = ksl
    w1 = P.sbuf("cw1", [128, 32, 128], BF16)
    w2 = P.sbuf("cw2", [128, 128], BF16)
    peT = P.sbuf("peT", [128, 32], BF16)
    b1 = P.sbuf("cb1", [128, 1], F32)
    hh = P.sbuf("chh", [128, 512], F32)
    h2 = P.sbuf("ch2", [128, 512], F32)
    gl = P.sbuf("cgl", [128, NCP], BF16)
    P.op("dve", lambda h: h.memset(gl[:, :], 0.0), writes=[gl])
    for kv in range(2):
        P.dma("pool", w1[:, :, :], w1_d[kv].rearrange("(j p) o -> p j o", p=128), writes=[w1])
        P.dma("pool", w2[:, :], w2_d[kv], writes=[w2])
        P.dma("pool", peT[:, :], pe_d[kv], writes=[peT])
        P.op("pe", lambda h: None, reads=[], writes=[]) if False else None
        for j in range(32):
            P.op("pe", lambda h, j=j: h.matmul(MB[:, 0:1], w1[:, j, :], peT[:, j:j + 1], start=(j == 0), stop=(j == 31)),
                 reads=[w1, peT], writes=[MB])
        P.op("act", lambda h: h.activation(out=b1[:, :], in_=MB[:, 0:1], func=AF.Copy), reads=[MB], writes=[b1])
        for g in range(2):
            P.dma("sp", src[:, :], (kc_d if kv == 0 else vc_d)[g], writes=[src])
            for c0 in range(0, NCMP, 512):
                n = min(512, NCMP - c0)
                acc = Sb[(c0 // 512) % 3]
                for j in range(32):
                    P.op("pe", lambda h, j=j, c0=c0, n=n, acc=acc: h.matmul(
                        acc[:, 0:n], w1[:, j, :], src[:, 16 * c0 + j:16 * (c0 + n - 1) + j + 1:16], start=(j == 0), stop=(j == 31)),
                         reads=[w1, src], writes=[acc])
                P.op("act", lambda h, n=n, acc=acc: h.activation(out=hh[:, 0:n], in_=acc[:, 0:n], func=AF.Identity, bias=b1[:, 0:1]),
                     reads=[acc, b1], writes=[hh])
                P.op("dve", lambda h, n=n: h.tensor_tensor(h2[:, 0:n], hh[:, 0:n], hh[:, 0:n], ALU.mult), reads=[hh], writes=[h2])
                P.op("dve", lambda h, n=n: h.tensor_tensor(h2[:, 0:n], h2[:, 0:n], hh[:, 0:n], ALU.mult), reads=[hh, h2], writes=[h2])
                P.op("dve", lambda h, n=n: h.scalar_tensor_tensor(h2[:, 0:n], h2[:, 0:n], 0.044715, hh[:, 0:n], ALU.mult, ALU.add),
                     reads=[hh, h2], writes=[h2])
                P.op("act", lambda h, n=n: h.activation(out=h2[:, 0:n], in_=h2[:, 0:n], func=AF.Tanh, scale=0.7978845608028654),
                     reads=[h2], writes=[h2])
                P.op("dve", lambda h, n=n: h.tensor_scalar(h2[:, 0:n], h2[:, 0:n], 1.0, 0.5, ALU.add, ALU.mult), reads=[h2], writes=[h2])
                P.op("dve", lambda h, n=n, c0=c0: h.tensor_tensor(gl[:, c0:c0 + n], h2[:, 0:n], hh[:, 0:n], ALU.mult),
                     reads=[hh, h2], writes=[gl])
            if kv == 0:
                for c0 in range(0, NCP, 512):
                    n = min(512, NCP - c0)
                    P.op("pe", lambda h, c0=c0, n=n: h.matmul(MB[:, 0:n], w2[:, :], gl[:, c0:c0 + n], start=True, stop=True),
                         reads=[w2, gl], writes=[MB])
                    P.op("act", lambda h, c0=c0, n=n, g=g: h.activation(out=kcT[:, g, c0:c0 + n], in_=MB[:, 0:n], func=AF.Copy),
                         reads=[MB], writes=[kcT])
            else:
                for ct in range(NCT):
                    P.op("pe", lambda h, ct=ct: h.matmul(MB[:, 0:128], gl[:, ct * 128:(ct + 1) * 128], w2[:, :], start=True, stop=True),
                         reads=[w2, gl], writes=[MB])
                    P.op("act", lambda h, ct=ct, g=g: h.activation(out=vcA[:, g, ct, 0:128], in_=MB[:, 0:128], func=AF.Copy),
                         reads=[MB], writes=[vcA])

    vsl = P.sbuf("vsl", [128, NKT, 129], BF16)
    kw = P.sbuf("kw", [128, WH * 128 + NT], BF16)
    vw = P.sbuf("vw", [128, WH + NQ, 129], BF16)
    kb = P.sbuf("kb", [128, 128 + NT], BF16)
    vb = P.sbuf("vb", [128, 1 + NQ, 129], BF16)
    for t_ in (vsl, vw, vb):
        P.op("dve", lambda h, t_=t_: h.memset(t_[:, :, 128:129], 1.0), writes=[t_])
    Qa = [P.sbuf(f"Qa{i}", [128, 4, 128], BF16) for i in range(2)]
    Qb = [P.sbuf(f"Qb{i}", [128, 4, 128], BF16) for i in range(2)]
    cmbs = [P.sbuf(f"cmbs{i}", [128, NCT, 128], BF16) for i in range(2)]
    cmb4 = P.sbuf("cmb4", [128, NCT, 4, 128], BF16)
    aad = [P.sbuf(f"aad{i}", [128, NBLK], F32) for i in range(2)]
    gt = [P.sbuf(f"gt{i}", [128, 24], F32) for i in range(2)]
    Pc = P.sbuf("Pc", [128, NCT, 512], BF16)
    Pt = [P.sbuf(f"Pt{i}", [128, 512], BF16) for i in range(3)]
    Ocs = P.sbuf("Ocs", [128, 4, 129], F32)
    Oss = P.sbuf("Oss", [128, 4, 129], F32)
    Ows = P.sbuf("Ows", [128, 4, 129], F32)
    Obs = P.sbuf("Obs", [128, 4, 129], F32)
    rl = P.sbuf("rl", [128, 16], F32)
    cf = P.sbuf("cf", [128, 16], F32)
    score = P.sbuf("score", [128, NBLK], F32)
    work = P.sbuf("work", [128, NBLK], F32)
    m8 = P.sbuf("m8", [128, 16], F32)
    mbias = P.sbuf("mbias", [128, NBLK], F32)
    mbT4 = P.sbuf("mbT4", [PB, NBH, 4, 128], BF16)
    orow = [P.sbuf(f"orow{i}", [128, 2, 512], BF16) for i in range(2)]
    otmp = P.sbuf("otmp", [128, 128], F32)
    state = {"s": 0, "p": 0}

    def score_exp(lhsT, lres, Q, biases):
        S = Sb[state["s"] % 3]
        state["s"] += 1
        nb = len(biases)
        P.op("pe", lambda h: h.matmul(S[:, :], lhsT, Q[:, :, :].rearrange("p a q -> p (a q)"), start=True, stop=(nb == 0)),
             reads=[lres, Q], writes=[S])
        for i, (bl, br, rr) in enumerate(biases):
            P.op("pe", lambda h, bl=bl, br=br, i=i: h.matmul(S[:, :], bl, br, start=False, stop=(i == nb - 1)), reads=rr, writes=[S])
        return S

    def do_exp(S, out_ap, ores):
        P.op("act", lambda h: h.activation(out=out_ap, in_=S[:, :], func=AF.Exp, scale=SCALE), reads=[S], writes=[ores])

    zer = P.sbuf("zer", [128, 128], BF16)
    P.op("dve", lambda h: h.memset(zer[:, :], 0.0), writes=[zer])

    def pv(pt_ap, pres, v_ap, vres, first, last):
        if first:
            for Ob in (OA, OB):
                P.op("pe", lambda h, Ob=Ob: h.matmul(Ob[:, 0:258], zer[:, :], mk4[:, 0, 0:258], start=True, stop=False),
                     reads=[zer, mk4], writes=[Ob])
        for hh_ in range(4):
            Ob = OA if hh_ < 2 else OB
            P.op("pe", lambda h, hh_=hh_, Ob=Ob: h.matmul(
                Ob[:, 0:258].rearrange("p (a d) -> p a d", a=2)[:, hh_ % 2, :], pt_ap[:, hh_ * 128:(hh_ + 1) * 128], v_ap,
                start=False, stop=(last and hh_ % 2 == 1)), reads=[pres, vres], writes=[Ob])

    def evac(dst):
        P.op("dve", lambda h: h.tensor_copy(dst[:, 0:2, :], OA[:, 0:258].rearrange("p (a d) -> p a d", a=2)), reads=[OA], writes=[dst])
        P.op("act", lambda h: h.activation(out=dst[:, 2:4, :], in_=OB[:, 0:258].rearrange("p (a d) -> p a d", a=2), func=AF.Copy),
             reads=[OB], writes=[dst])

    def next_pt():
        pt = Pt[state["p"] % 3]
        state["p"] += 1
        return pt

    it = 0
    for g in range(2):
        P.dma("sp", ksl[:, :], ks_d[g], writes=[ksl])
        P.dma("sp", vsl[:, :, 0:128], vs_d[g].rearrange("(k p) d -> p k d", p=128), writes=[vsl])
        P.dma("sp", kw[:, :], kw_d[g], writes=[kw])
        P.dma("sp", vw[:, :, 0:128], vw_d[g].rearrange("(k p) d -> p k d", p=128), writes=[vw])
        P.dma("sp", kb[:, :], kb_d[g], writes=[kb])
        P.dma("sp", vb[:, :, 0:128], vb_d[g].rearrange("(k p) d -> p k d", p=128), writes=[vb])
        for qi in range(NQ):
            b = it % 2
            it += 1
            q0 = qi * 128
            qa, qb, cm, aa, gg, orw = Qa[b], Qb[b], cmbs[b], aad[b], gt[b], orow[b]
            P.dma("sp", qa[:, :, :], qa_d[4 * g:4 * g + 4, :, q0:q0 + 128].rearrange("h p t -> p h t"), writes=[qa])
            P.dma("sp", qb[:, :, :], qb_d[4 * g:4 * g + 4, :, q0:q0 + 128].rearrange("h p t -> p h t"), writes=[qb])
            P.dma("sp", cm[:, :, :], cmb_d[:, :, q0:q0 + 128].rearrange("k p t -> p k t"), writes=[cm])
            P.dma("sp", aa[:, :], aadd_d[q0:q0 + 128, :], writes=[aa])
            P.dma("sp", gg[:, :], g_d[q0:q0 + 128, :], writes=[gg])
            for hh_ in range(4):
                P.op("pool", lambda h, hh_=hh_, cm=cm: h.tensor_copy(cmb4[:, :, hh_, :], cm[:, :, :]), reads=[cm], writes=[cmb4])
            for ct in range(NCT):
                S = score_exp(kcT[:, g, ct * 128:(ct + 1) * 128], kcT, qa,
                              [(identb[:, :], cmb4[:, ct, :, :].rearrange("p a q -> p (a q)"), [identb, cmb4])])
                do_exp(S, Pc[:, ct, :], Pc)
            for ct in range(NCT):
                pv(Pc[:, ct, :], Pc, vcA[:, g, ct, :], vcA, ct == 0, ct == NCT - 1)
            for hh_ in range(4):
                Ib = IA if hh_ < 2 else IB
                for ct in range(NCT):
                    P.op("pe", lambda h, hh_=hh_, ct=ct, Ib=Ib: h.matmul(
                        Ib[:, (hh_ % 2) * 256:(hh_ % 2) * 256 + NBLK], Pc[:, ct, hh_ * 128:(hh_ + 1) * 128], c2s[:, ct, :],
                        start=(ct == 0), stop=(ct == NCT - 1)), reads=[Pc, c2s], writes=[Ib])
            evac(Ocs)
            P.op("dve", lambda h: h.tensor_scalar(rl[:, 0:4], Ocs[:, :, 128], 1e-30, None, ALU.max), reads=[Ocs], writes=[rl])
            P.op("dve", lambda h: h.reciprocal(rl[:, 0:4], rl[:, 0:4]), reads=[rl], writes=[rl])
            P.op("dve", lambda h, aa=aa: h.scalar_tensor_tensor(score[:, :], IA[:, 0:NBLK], rl[:, 0:1], aa[:, :], ALU.mult, ALU.add),
                 reads=[IA, rl, aa], writes=[score])
            P.op("dve", lambda h: h.scalar_tensor_tensor(score[:, :], IA[:, 256:256 + NBLK], rl[:, 1:2], score[:, :], ALU.mult, ALU.add),
                 reads=[IA, rl, score], writes=[score])
            P.op("dve", lambda h: h.scalar_tensor_tensor(score[:, :], IB[:, 0:NBLK], rl[:, 2:3], score[:, :], ALU.mult, ALU.add),
                 reads=[IB, rl, score], writes=[score])
            P.op("dve", lambda h: h.scalar_tensor_tensor(score[:, :], IB[:, 256:256 + NBLK], rl[:, 3:4], score[:, :], ALU.mult, ALU.add),
                 reads=[IB, rl, score], writes=[score])
            P.op("dve", lambda h: h.max(out=m8[:, 0:8], in_=score[:, :]), reads=[score], writes=[m8])
            P.op("dve", lambda h: h.match_replace(out=work[:, :], in_to_replace=m8[:, 0:8], in_values=score[:, :], imm_value=-3.0e38),
                 reads=[score, m8], writes=[work])
            P.op("dve", lambda h: h.max(out=m8[:, 8:16], in_=work[:, :]), reads=[work], writes=[m8])
            P.op("dve", lambda h: h.tensor_scalar(m8[:, 15:16], m8[:, 15:16], -1.0e29, None, ALU.max), reads=[m8], writes=[m8])
            P.op("dve", lambda h: h.tensor_scalar(mbias[:, :], score[:, :], m8[:, 15:16], NEG, ALU.is_lt, ALU.mult),
                 reads=[score, m8], writes=[mbias])
            for hf in range(NBH):
                P.op("pe", lambda h, hf=hf: h.transpose(MB[0:PB, hf * 128:(hf + 1) * 128], mbias[:, hf * PB:(hf + 1) * PB], identf[:, :]),
                     reads=[mbias, identf], writes=[MB])
            for hh_ in range(4):
                P.op("act", lambda h, hh_=hh_: h.activation(
                    out=mbT4[:, :, hh_, :], in_=MB[0:PB, 0:NBH * 128].rearrange("p (a q) -> p a q", a=NBH), func=AF.Copy),
                     reads=[MB], writes=[mbT4])
            kts = list(range(0, qi + 1)) + list(range(NOWN, NKT))
            for i, kt in enumerate(kts):
                biases = [(ind[:, kt % KPH, :], mbT4[:, kt // KPH, :, :].rearrange("p a q -> p (a q)"), [ind, mbT4])]
                if kt == qi:
                    biases.append((identb[:, :], mk4[:, 0, :], [identb, mk4]))
                S = score_exp(ksl[:, kt * 128:(kt + 1) * 128], ksl, qa, biases)
                pt = next_pt()
                do_exp(S, pt[:, :], pt)
                pv(pt, pt, vsl[:, kt, :], vsl, i == 0, i == len(kts) - 1)
            evac(Oss)
            for i in range(WH + 1):
                kt = qi + i
                biases = []
                if i == 0:
                    biases.append((identb[:, :], mk4[:, 1, :], [identb, mk4]))
                if i == WH:
                    biases.append((identb[:, :], mk4[:, 0, :], [identb, mk4]))
                if kt < WH:
                    biases.append((identb[:, :], mk4[:, 2, :], [identb, mk4]))
                S = score_exp(kw[:, kt * 128:(kt + 1) * 128], kw, qa, biases)
                pt = next_pt()
                do_exp(S, pt[:, :], pt)
                pv(pt, pt, vw[:, kt, :], vw, i == 0, i == WH)
            evac(Ows)
            for i in range(2):
                kt = qi + i
                biases = [(identb[:, :], mk4[:, 1 if i == 0 else 0, :], [identb, mk4])]
                if kt < 1:
                    biases.append((identb[:, :], mk4[:, 2, :], [identb, mk4]))
                S = score_exp(kb[:, kt * 128:(kt + 1) * 128], kb, qb, biases)
                pt = next_pt()
                do_exp(S, pt[:, :], pt)
                pv(pt, pt, vb[:, kt, :], vb, i == 0, i == 1)
            evac(Obs)
            P.op("dve", lambda h: h.reciprocal(rl[:, 4:8], Oss[:, :, 128]), reads=[Oss], writes=[rl])
            P.op("dve", lambda h: h.reciprocal(rl[:, 8:12], Ows[:, :, 128]), reads=[Ows], writes=[rl])
            P.op("dve", lambda h, g=g: h.tensor_tensor(rl[:, 12:16], Obs[:, :, 128], sk[:, 4 * g:4 * g + 4], ALU.add), reads=[Obs, sk], writes=[rl])
            P.op("dve", lambda h: h.reciprocal(rl[:, 12:16], rl[:, 12:16]), reads=[rl], writes=[rl])
            ggv = gg[:, 12 * g:12 * g + 12].rearrange("p (a j) -> p a j", j=3)
            for j in range(3):
                P.op("dve", lambda h, j=j, ggv=ggv: h.tensor_tensor(cf[:, 4 * j:4 * j + 4], rl[:, 4 * j:4 * j + 4], ggv[:, :, j], ALU.mult),
                     reads=[rl, gg], writes=[cf])
            for hh_ in range(4):
                P.op("dve", lambda h, hh_=hh_: h.tensor_scalar(otmp[:, :], Ocs[:, hh_, 0:128], cf[:, hh_:hh_ + 1], None, ALU.mult),
                     reads=[Ocs, cf], writes=[otmp])
                P.op("dve", lambda h, hh_=hh_: h.scalar_tensor_tensor(otmp[:, :], Oss[:, hh_, 0:128], cf[:, 4 + hh_:5 + hh_], otmp[:, :], ALU.mult, ALU.add),
                     reads=[Oss, cf, otmp], writes=[otmp])
                P.op("dve", lambda h, hh_=hh_, orw=orw: h.scalar_tensor_tensor(
                    orw[:, 0, hh_ * 128:(hh_ + 1) * 128], Ows[:, hh_, 0:128], cf[:, 8 + hh_:9 + hh_], otmp[:, :], ALU.mult, ALU.add),
                     reads=[Ows, cf, otmp], writes=[orw])
                P.op("pool", lambda h, hh_=hh_, orw=orw: h.tensor_scalar(
                    orw[:, 1, hh_ * 128:(hh_ + 1) * 128], Obs[:, hh_, 0:128], rl[:, 12 + hh_:13 + hh_], None, ALU.mult),
                     reads=[Obs, rl], writes=[orw])
            P.dma("sp", o_d[q0:q0 + 128, g * 512:(g + 1) * 512], orw[:, 0, :], reads=[orw])
            P.dma("sp", o_d[q0:q0 + 128, 1024 + g * 512:1024 + (g + 1) * 512], orw[:, 1, :], reads=[orw])
    P.finish()
    return nc


def even_consts(NT, SEQ_, ch):
    NBLK = SEQ_ // 64
    NCMP = (SEQ_ - 32) // 16 + 1
    NCP = ((NCMP + 127) // 128) * 128
    PB = min(128, NBLK)
    rot = (np.arange(NBLK) + ch * (NT // 64)) % NBLK
    t_abs = ch * NT + np.arange(NT)
    cur = t_abs // 64
    blk = rot[None, :]
    valid = blk * 64 <= t_abs[:, None]
    aadd = np.where(valid, 0.0, -1.0e30).astype(np.float32)
    aadd = np.where(blk == cur[:, None] - 1, 2.0e30, aadd)
    aadd = np.where(blk == cur[:, None], 3.0e30, aadd)
    aadd = np.where(blk == 0, 1.0e30, aadd).astype(np.float32)
    aadd = np.where((blk == 0) & (cur[:, None] == 0), 3.0e30, aadd).astype(np.float32)
    aadd = np.where((blk == 0) & (cur[:, None] == 1), 2.0e30, aadd).astype(np.float32)
    pos = np.arange(NCMP)[:, None] * 16 + np.arange(32)[None, :]
    owner = pos // 64
    c2s = (owner[:, :, None] == np.arange(NBLK)[None, None, :]).sum(1) / 32.0
    c2s_p = np.zeros((NCP, NBLK), np.float32)
    c2s_p[:NCMP] = c2s[:, rot]
    cend = np.arange(NCP) * 16 + 31
    cmb = np.where(cend[:, None] <= t_abs[None, :], 0.0, NEG).astype(ml_dtypes.bfloat16).reshape(NCP // 128, 128, NT)
    ind = np.zeros((PB // 2, PB, 128), np.float32)
    for k in range(PB // 2):
        ind[k, 2 * k, 0:64] = 1.0
        ind[k, 2 * k + 1, 64:128] = 1.0
    lt4 = np.tile(_LT, (1, 4))
    uts = np.where(np.arange(128)[:, None] > np.arange(128)[None, :], 0.0, NEG).astype(np.float32)
    uts4 = np.tile(uts, (1, 4))
    hb4 = np.full((128, 512), NEG if ch == 0 else 0.0, np.float32)
    return {"aadd": aadd, "c2s": c2s_p, "cmb": cmb, "ind": ind, "masks4": np.stack([lt4, uts4, hb4]), "ident": _IDENT}


BF = ml_dtypes.bfloat16
NCH = SEQ // CH


def _spmd(nc, in_maps):
    return run_bass_kernel_spmd(nc, in_maps, core_ids=list(range(len(in_maps)))).results


def _halo_cols(full, ch, H):
    if ch == 0:
        z = np.zeros(full.shape[:-1] + (H,), full.dtype)
        return np.concatenate([z, full[..., 0:CH]], axis=-1)
    return np.ascontiguousarray(full[..., ch * CH - H:(ch + 1) * CH])


def _halo_rows(full, ch, H):
    if ch == 0:
        z = np.zeros((H,) + full.shape[1:], full.dtype)
        return np.concatenate([z, full[0:CH]], axis=0)
    return np.ascontiguousarray(full[ch * CH - H:(ch + 1) * CH])


_EVEN_CONSTS = {}


def _even_consts(ch):
    if ch not in _EVEN_CONSTS:
        _EVEN_CONSTS[ch] = even_consts(CH, SEQ, ch)
    return _EVEN_CONSTS[ch]


def even_layer(x_sh, pos_sh, w_in, w_o, pe_k, pe_v, ck1, ck2, cv1, cv2, sinks):
    nc = build_P(CH, w_in.shape[1], EVEN_ROPE, EVEN_V, EVEN_GATE)
    res = _spmd(nc, [{"x": x_sh[c], "pos": pos_sh[c], "w_in": w_in, "invf": _INVF128, "rotT": _ROTT, "ident": _IDENT}
                     for c in range(NCORES)])
    nc = build_A_even(CH, SEQ)
    peT = np.ascontiguousarray(np.stack([pe_k.T, pe_v.T]))
    cw1 = np.ascontiguousarray(np.stack([ck1, cv1]))
    cw2 = np.ascontiguousarray(np.stack([ck2, cv2]))
    in_maps = []
    for b in range(2):
        qk = np.concatenate([res[b * NCH + ch]["qkT"] for ch in range(NCH)], axis=2)
        v = np.concatenate([res[b * NCH + ch]["v"] for ch in range(NCH)], axis=0)
        vcT = np.ascontiguousarray(v[:, 0:256].T.reshape(2, 128, SEQ))
        for ch in range(NCH):
            r = res[b * NCH + ch]
            m = dict(_even_consts(ch))
            m.update({
                "qTa": np.ascontiguousarray(r["qkT"][0:8]), "qTb": np.ascontiguousarray(r["qkT"][14:22]),
                "kcT": np.ascontiguousarray(qk[8:10]), "vcT": vcT,
                "ksT": np.roll(qk[10:12], -ch * CH, axis=2),
                "vs": np.ascontiguousarray(np.roll(v[:, 256:512], -ch * CH, axis=0).reshape(SEQ, 2, 128).transpose(1, 0, 2)),
                "kwT": _halo_cols(qk[12:14], ch, 512),
                "vw": np.ascontiguousarray(_halo_rows(v[:, 512:768], ch, 512).reshape(512 + CH, 2, 128).transpose(1, 0, 2)),
                "kbT": _halo_cols(qk[22:24], ch, 128),
                "vb": np.ascontiguousarray(_halo_rows(v[:, 768:1024], ch, 128).reshape(128 + CH, 2, 128).transpose(1, 0, 2)),
                "gates": r["gates"], "sinks": sinks, "peT": peT, "cw1": cw1, "cw2": cw2,
            })
            in_maps.append(m)
    res2 = _spmd(nc, in_maps)
    return [r["o"] for r in res2]


ODD_DILS = (1, 4, 16)
ODD_HMAX = 2048


def odd_layer(x_sh, pos_sh, w_in):
    nc = build_P(CH, w_in.shape[1], ODD_ROPE, ODD_V, None)
    res = _spmd(nc, [{"x": x_sh[c], "pos": pos_sh[c], "w_in": w_in, "invf": _INVF128, "rotT": _ROTT, "ident": _IDENT}
                     for c in range(NCORES)])
    nc = build_A_odd(CH, ODD_HMAX, ODD_DILS, 16)
    in_maps = []
    for b in range(2):
        qk = np.concatenate([res[b * NCH + ch]["qkT"] for ch in range(NCH)], axis=2).reshape(3, 2, 16, 128, SEQ)
        v = np.concatenate([res[b * NCH + ch]["v"] for ch in range(NCH)], axis=0)
        for ch in range(NCH):
            r = res[b * NCH + ch]
            in_maps.append({
                "qT": np.ascontiguousarray(r["qkT"].reshape(3, 2, 16, 128, CH)[:, 0]),
                "kT": _halo_cols(qk[:, 1], ch, ODD_HMAX),
                "v": np.ascontiguousarray(_halo_rows(v, ch, ODD_HMAX).reshape(ODD_HMAX + CH, 3, 2048).transpose(1, 0, 2)),
                "masks": odd_masks(ch == 0), "ident": _IDENT,
            })
    res2 = _spmd(nc, in_maps)
    return [r["oT"] for r in res2]


def kernel(x, positions, e_w_in, e_w_o, nsa_pe_k, nsa_pe_v, nsa_ck_w1, nsa_ck_w2, nsa_cv_w1, nsa_cv_w2, swa_sinks,
           o_w_in, o_w_o, ln1_g, ln1_b, mlp_w1, mlp_w2, ln2_g, ln2_b):
    f32 = lambda a: np.ascontiguousarray(np.asarray(a, dtype=np.float32))
    x = f32(x)
    positions = np.ascontiguousarray(np.asarray(positions, dtype=np.int32))
    x_sh = [np.ascontiguousarray(x[c // NCH, (c % NCH) * CH:(c % NCH + 1) * CH]) for c in range(NCORES)]
    pos_sh = [np.ascontiguousarray(positions[c // NCH, (c % NCH) * CH:(c % NCH + 1) * CH]) for c in range(NCORES)]
    for layer in range(DEPTH):
        i = layer // 2
        if layer % 2 == 0:
            o_sh = even_layer(x_sh, pos_sh, f32(e_w_in[i]), f32(e_w_o[i]), f32(nsa_pe_k[i]), f32(nsa_pe_v[i]),
                              f32(nsa_ck_w1[i]), f32(nsa_ck_w2[i]), f32(nsa_cv_w1[i]), f32(nsa_cv_w2[i]), f32(swa_sinks[i]))
            w_o = f32(e_w_o[i])
        else:
            o_sh = odd_layer(x_sh, pos_sh, f32(o_w_in[i]))
            w_o = f32(o_w_o[i])
        lnp = f32(np.stack([ln1_g[layer], ln1_b[layer], ln2_g[layer], ln2_b[layer]]))
        x_sh = run_M(o_sh, x_sh, w_o, f32(mlp_w1[layer]), f32(mlp_w2[layer]), lnp, o_fm=(layer % 2 == 1))
    out = np.empty((2, SEQ, D), np.float32)
    for c in range(NCORES):
        out[c // NCH, (c % NCH) * CH:(c % NCH + 1) * CH] = x_sh[c]
    return out
```
